# Optimizing a Trainium2 kernel written in Bass

```python
import math
import jax, jax.numpy as jnp
from jax import lax
import numpy as np

D_MODEL = 1024
BATCH = 2
SEQ = 16384
DEPTH = 1

GRID_W = 64
CTX_LEN = 256
N_ADA = 9
FFN_HIDDEN = 2816
D_HYENA = 512
HYENA_N_PROJ = 3
SHORT_CONV_W = 3
FILTER_EMB_BANDS = 16
FILTER_EMB_DIM = 1 + 2 * FILTER_EMB_BANDS
FILTER_HIDDEN = 64
FILTER_INNER = 2
FILTER_OUT_SCALE = 0.02 * FILTER_HIDDEN ** -0.5
DECAY_TARGET = 1e-2
FAST_DECAY_PCT = 0.3
SLOW_DECAY_PCT = 1.5
N_DIFF_HEADS = 4
DIFF_HEAD_DIM = 64
D_DIFF = N_DIFF_HEADS * 2 * DIFF_HEAD_DIM
D_MIX = D_HYENA + D_DIFF
D_IN = HYENA_N_PROJ * D_HYENA + 3 * D_DIFF
ROPE_BASE = 10000.0
Q_BLOCK = 128
RMS_EPS = 1e-6
SUBLN_EPS = 1e-5

kernel_name = 'hybrid_hyena_diffattn_dit_layer'


def rms_norm(x, g, eps=RMS_EPS):
    xf = x.astype(jnp.float32)
    y = xf * lax.rsqrt(jnp.mean(xf * xf, axis=-1, keepdims=True) + eps)
    return (y * g.astype(jnp.float32)).astype(x.dtype)


def adaln(x, g, shift, scale):
    return rms_norm(x, g) * (1.0 + scale) + shift


def ffn_half_step(s, g, shift, scale, gate, w_gate, w_up, w_down):
    u = adaln(s, g, shift, scale)
    return s + 0.5 * gate * ((jax.nn.silu(u @ w_gate) * (u @ w_up)) @ w_down)


def centred_short_conv(z, w, b):
    zp = jnp.pad(z, ((0, 0), (1, 1), (0, 0)))
    return zp[:, :-2] * w[0] + zp[:, 1:-1] * w[1] + zp[:, 2:] * w[2] + b


def hyena_filter(L, w1, b1, w_inner, b_inner, freq, w_out):
    t = jnp.linspace(0.0, 1.0, L, dtype=jnp.float32)[:, None]
    omega = (2.0 * math.pi / L) * jnp.arange(L, dtype=jnp.float32)[:, None]
    bands = jnp.linspace(1e-4, FILTER_EMB_BANDS - 1, FILTER_EMB_BANDS, dtype=jnp.float32)[None, :]
    ang = omega * bands
    emb = jnp.concatenate([t, jnp.cos(ang), -jnp.sin(ang)], axis=-1)
    freq = freq.astype(jnp.float32)
    h = jnp.sin(freq * (emb @ w1.astype(jnp.float32) + b1.astype(jnp.float32)))
    for i in range(FILTER_INNER):
        h = jnp.sin(freq * (h @ w_inner[i].astype(jnp.float32) + b_inner[i].astype(jnp.float32)))
    h = (h @ w_out.astype(jnp.float32)).reshape(L, 2, D_HYENA)
    min_decay = math.log(DECAY_TARGET) / SLOW_DECAY_PCT
    max_decay = math.log(DECAY_TARGET) / FAST_DECAY_PCT
    deltas = jnp.abs(jnp.linspace(min_decay, max_decay, D_HYENA, dtype=jnp.float32))
    return h * jnp.exp(-t * deltas)[:, None, :]


def bidirectional_fft_conv(u, h, bias):
    L = u.shape[1]
    h_fwd, h_bwd = h[:, 0], h[:, 1]
    k_full = jnp.concatenate([h_fwd, jnp.zeros_like(h_fwd[:1]), h_bwd[:0:-1]], axis=0)
    k_f = jnp.fft.rfft(k_full, n=2 * L, axis=0)
    uf = u.astype(jnp.float32)
    u_f = jnp.fft.rfft(uf, n=2 * L, axis=1)
    y = jnp.fft.irfft(u_f * k_f[None], n=2 * L, axis=1)[:, :L]
    return (y + uf * bias.astype(jnp.float32)).astype(u.dtype)


def hyena_mixer(z, conv_w, conv_b, w1, b1, w_inner, b_inner, freq, w_out, bias):
    L = z.shape[1]
    z = centred_short_conv(z, conv_w, conv_b)
    x0, x1, v = jnp.split(z, HYENA_N_PROJ, axis=-1)
    h = hyena_filter(L, w1, b1, w_inner, b_inner, freq, w_out)
    return x0 * bidirectional_fft_conv(x1 * v, h, bias)


def axial_rope_tables(n_lat):
    rows = n_lat // GRID_W
    row = jnp.broadcast_to(jnp.arange(rows, dtype=jnp.float32)[:, None], (rows, GRID_W)).reshape(-1)
    col = jnp.broadcast_to(jnp.arange(GRID_W, dtype=jnp.float32)[None, :], (rows, GRID_W)).reshape(-1)
    axis_dim = DIFF_HEAD_DIM // 2
    inv_freq = ROPE_BASE ** (-jnp.arange(0, axis_dim, 2, dtype=jnp.float32) / axis_dim)
    ang = jnp.stack([row[:, None] * inv_freq, col[:, None] * inv_freq], axis=1)
    return jnp.cos(ang), jnp.sin(ang)


def apply_axial_rope(x, cos, sin):
    b_, n_, h_, s_, d_ = x.shape
    xr = x.astype(jnp.float32).reshape(b_, n_, h_, s_, 2, 2, d_ // 4)
    x1, x2 = xr[..., 0, :], xr[..., 1, :]
    cs, sn = cos[None, :, None, None], sin[None, :, None, None]
    out = jnp.stack([x1 * cs - x2 * sn, x2 * cs + x1 * sn], axis=-2)
    return out.reshape(x.shape).astype(x.dtype)


def diff_attention(q, k, v, lam):
    s = jnp.einsum('bqhsd,bkhsd->bhsqk', q, k).astype(jnp.float32) * (DIFF_HEAD_DIM ** -0.5)
    p = jax.nn.softmax(s, axis=-1)
    a = p[:, :, 0] - lam * p[:, :, 1]
    return jnp.einsum('bhqk,bkhe->bqhe', a.astype(v.dtype), v)


def diff_head_out(o, g, lam_init):
    o = rms_norm(o, g, SUBLN_EPS) * (1.0 - lam_init)
    return o.reshape(o.shape[0], o.shape[1], D_DIFF)


def setup_inputs(seed: int = 0) -> dict:
    key = jax.random.key(seed)
    ks = jax.random.split(key, 24)

    def nrm(k, shape, s):
        return s * jax.random.normal(k, shape, jnp.float32)

    return {
        'x': nrm(ks[0], (BATCH, SEQ, D_MODEL), 1.0),
        'c': nrm(ks[1], (BATCH, D_MODEL), 1.0),
        'ctx': nrm(ks[2], (BATCH, CTX_LEN, D_MODEL), 1.0),
        'c_ctx': nrm(ks[3], (D_MODEL,), 1.0),
        'ada_w': nrm(ks[4], (DEPTH, D_MODEL, N_ADA * D_MODEL), 0.5 * D_MODEL ** -0.5),
        'ada_b': nrm(ks[5], (DEPTH, N_ADA * D_MODEL), 0.01),
        'norm_g': 1.0 + nrm(ks[6], (DEPTH, 3, D_MODEL), 0.02),
        'ffn_w_gate': nrm(ks[7], (DEPTH, 2, D_MODEL, FFN_HIDDEN), D_MODEL ** -0.5),
        'ffn_w_up': nrm(ks[8], (DEPTH, 2, D_MODEL, FFN_HIDDEN), D_MODEL ** -0.5),
        'ffn_w_down': nrm(ks[9], (DEPTH, 2, FFN_HIDDEN, D_MODEL), FFN_HIDDEN ** -0.5),
        'w_in': nrm(ks[10], (DEPTH, D_MODEL, D_IN), D_MODEL ** -0.5),
        'w_out': nrm(ks[11], (DEPTH, D_MIX, D_MODEL), D_MIX ** -0.5),
        'hyena_conv_w': nrm(ks[12], (DEPTH, SHORT_CONV_W, HYENA_N_PROJ * D_HYENA), SHORT_CONV_W ** -0.5),
        'hyena_conv_b': nrm(ks[13], (DEPTH, HYENA_N_PROJ * D_HYENA), 0.01),
        'filt_w1': nrm(ks[14], (DEPTH, FILTER_EMB_DIM, FILTER_HIDDEN), FILTER_EMB_DIM ** -0.5),
        'filt_b1': nrm(ks[15], (DEPTH, FILTER_HIDDEN), 0.1),
        'filt_w_inner': nrm(ks[16], (DEPTH, FILTER_INNER, FILTER_HIDDEN, FILTER_HIDDEN), FILTER_HIDDEN ** -0.5),
        'filt_b_inner': nrm(ks[17], (DEPTH, FILTER_INNER, FILTER_HIDDEN), 0.1),
        'filt_sin_freq': 1.0 + nrm(ks[18], (DEPTH, FILTER_HIDDEN), 0.02),
        'filt_w_out': nrm(ks[19], (DEPTH, FILTER_HIDDEN, 2 * D_HYENA), FILTER_OUT_SCALE),
        'hyena_bias': nrm(ks[20], (DEPTH, D_HYENA), 1.0),
        'diff_lambda': nrm(ks[21], (DEPTH, 4, DIFF_HEAD_DIM), 0.1),
        'diff_subln_g': 1.0 + nrm(ks[22], (DEPTH, 2 * DIFF_HEAD_DIM), 0.02),
        'final_g': 1.0 + nrm(ks[23], (D_MODEL,), 0.02),
    }


def reference(x, c, ctx, c_ctx, ada_w, ada_b, norm_g, ffn_w_gate, ffn_w_up, ffn_w_down, w_in, w_out,
              hyena_conv_w, hyena_conv_b, filt_w1, filt_b1, filt_w_inner, filt_b_inner, filt_sin_freq,
              filt_w_out, hyena_bias, diff_lambda, diff_subln_g, final_g):
    B, N, D = x.shape
    C = ctx.shape[1]
    H, d = N_DIFF_HEADS, DIFF_HEAD_DIM
    n_blk = N // Q_BLOCK
    hy_end = HYENA_N_PROJ * D_HYENA
    q_end = hy_end + D_DIFF
    cos, sin = axial_rope_tables(N)
    h_lat, h_ctx = x, ctx
    for layer in range(DEPTH):
        last = layer == DEPTH - 1
        lam_init = 0.8 - 0.6 * math.exp(-0.3 * layer)
        mod_lat = (jax.nn.silu(c) @ ada_w[layer] + ada_b[layer]).reshape(B, N_ADA, D).transpose(1, 0, 2)[:, :, None, :]
        mod_ctx = (jax.nn.silu(c_ctx) @ ada_w[layer] + ada_b[layer]).reshape(N_ADA, 1, 1, D)
        hyena_params = (hyena_conv_w[layer], hyena_conv_b[layer], filt_w1[layer], filt_b1[layer],
                        filt_w_inner[layer], filt_b_inner[layer], filt_sin_freq[layer], filt_w_out[layer],
                        hyena_bias[layer])

        ffn0 = (ffn_w_gate[layer, 0], ffn_w_up[layer, 0], ffn_w_down[layer, 0])
        h_lat = ffn_half_step(h_lat, norm_g[layer, 0], mod_lat[0], mod_lat[1], mod_lat[2], *ffn0)
        h_ctx = ffn_half_step(h_ctx, norm_g[layer, 0], mod_ctx[0], mod_ctx[1], mod_ctx[2], *ffn0)

        u_lat = adaln(h_lat, norm_g[layer, 1], mod_lat[3], mod_lat[4])
        u_ctx = adaln(h_ctx, norm_g[layer, 1], mod_ctx[3], mod_ctx[4])
        p_lat = u_lat @ w_in[layer]
        kv_ctx = u_ctx @ w_in[layer][:, q_end:]
        lp = diff_lambda[layer].astype(jnp.float32)
        lam = jnp.exp(jnp.sum(lp[0] * lp[1])) - jnp.exp(jnp.sum(lp[2] * lp[3])) + lam_init

        q_lat = apply_axial_rope(p_lat[..., hy_end:q_end].reshape(B, N, H, 2, d), cos, sin)
        k_lat = apply_axial_rope(p_lat[..., q_end:q_end + D_DIFF].reshape(B, N, H, 2, d), cos, sin)
        v_lat = p_lat[..., q_end + D_DIFF:].reshape(B, N, H, 2 * d)
        k_ctx = kv_ctx[..., :D_DIFF].reshape(B, C, H, 2, d)
        v_ctx = kv_ctx[..., D_DIFF:].reshape(B, C, H, 2 * d)
        k_all = jnp.concatenate([k_lat, k_ctx], axis=1)
        v_all = jnp.concatenate([v_lat, v_ctx], axis=1)
        q_blocks = q_lat.reshape(B, n_blk, Q_BLOCK, H, 2, d).swapaxes(0, 1)
        o_lat = lax.map(lambda qb: diff_attention(qb, k_all, v_all, lam), q_blocks)
        o_lat = o_lat.swapaxes(0, 1).reshape(B, N, H, 2 * d)
        y_lat = jnp.concatenate([hyena_mixer(p_lat[..., :hy_end], *hyena_params),
                                 diff_head_out(o_lat, diff_subln_g[layer], lam_init)], axis=-1)
        h_lat = h_lat + mod_lat[5] * (y_lat @ w_out[layer])

        ffn1 = (ffn_w_gate[layer, 1], ffn_w_up[layer, 1], ffn_w_down[layer, 1])
        if not last:
            p_ctx = u_ctx @ w_in[layer][:, :q_end]
            q_ctx = p_ctx[..., hy_end:].reshape(B, C, H, 2, d)
            o_ctx = diff_attention(q_ctx, k_ctx, v_ctx, lam)
            y_ctx = jnp.concatenate([hyena_mixer(p_ctx[..., :hy_end], *hyena_params),
                                     diff_head_out(o_ctx, diff_subln_g[layer], lam_init)], axis=-1)
            h_ctx = h_ctx + mod_ctx[5] * (y_ctx @ w_out[layer])
            h_ctx = ffn_half_step(h_ctx, norm_g[layer, 2], mod_ctx[6], mod_ctx[7], mod_ctx[8], *ffn1)

        h_lat = ffn_half_step(h_lat, norm_g[layer, 2], mod_lat[6], mod_lat[7], mod_lat[8], *ffn1)
    return rms_norm(h_lat, final_g)
```

```python
import math
import os
from contextlib import ExitStack

import numpy as np
import ml_dtypes

import concourse.bass as bass
import concourse.mybir as mybir
from concourse.bass_utils import run_bass_kernel_spmd

F32 = mybir.dt.float32
BF16 = mybir.dt.bfloat16
AF = mybir.ActivationFunctionType
ALU = mybir.AluOpType
NPBF = ml_dtypes.bfloat16

D = 1024
NT = 4096
L = 16384
NFFT = 32768
FH = 2816
NJ = 22
CTX = 256
C_LAT = 2
C_HR = 2 + NT
C_CTX = 4 + NT
NAUG = 4 + NT + CTX
RG = [[0, 1, 2, 3], [4, 5, 6, 7]]
PI = math.pi


class Buf:
    def __init__(self, t, name="", dram=False):
        self.t = t
        self.name = name
        self.dram = dram
        self.w = {}
        self.r = {}
        self.sem = None

    def __getitem__(self, k):
        return self.t[k]


class Prog:
    ENG = ["pe", "act", "dve", "pool", "sp"]

    def __init__(self, nc, es):
        self.nc = nc
        self.es = es
        self.esem = {n: es.enter_context(nc.semaphore("s_" + n)) for n in self.ENG}
        self.cc_sem = es.enter_context(nc.semaphore("s_cc"))
        self.sempool = [es.enter_context(nc.semaphore("d_%d" % i)) for i in range(92)]
        with nc.Block() as b0:
            def clr(h):
                for s_ in list(self.esem.values()) + [self.cc_sem] + self.sempool:
                    h.sem_clear(s_)
            b0.gpsimd(clr)

        self.cc_n = 0
        self.count = {n: 0 for n in self.ENG}
        self.seen = {n: {} for n in self.ENG}
        self.per = {n: [] for n in self.ENG}
        self.allsems = {}
        self.nsem = 0
        self.free_sems = []
        self.free_sw = []
        self.strict = False

    def _ev_merge(self, d, ev):
        k = id(ev[0])
        if k not in d or d[k][1] < ev[1]:
            d[k] = ev

    def need(self, en, ev, same_ok=True):
        s, v = ev
        if same_ok and s is self.esem[en] and not self.strict:
            return
        k = id(s)
        if self.seen[en].get(k, 0) >= v:
            return
        self.seen[en][k] = v
        self.per[en].append(("wait", s, v))

    def deps(self, en, reads, writes, waw=True, same_ok=True):
        for b in reads:
            for ev in b.w.values():
                self.need(en, ev, same_ok)
        for b in writes:
            if waw:
                for ev in b.w.values():
                    self.need(en, ev, same_ok)
            for ev in b.r.values():
                self.need(en, ev, same_ok)

    def mark(self, ev, reads, writes, append=False):
        k = id(ev[0])
        self.allsems[k] = ev
        for b in reads:
            self._ev_merge(b.r, ev)
        for b in writes:
            if append:
                self._ev_merge(b.w, ev)
            else:
                b.w = {k: ev}
                b.r = {}

    def op(self, en, meth, reads=(), writes=(), **kw):
        self.deps(en, reads, writes)
        self.count[en] += 1
        ev = (self.esem[en], self.count[en])
        self.per[en].append(("op", meth, self.esem[en], kw))
        self.mark(ev, reads, writes)

    def bufsem(self, b, sw=False):
        if b.sem is None:
            fl = self.free_sw if sw else self.free_sems
            if fl:
                b.sem = fl.pop()
            else:
                b.sem = [self.sempool[self.nsem], 0, sw]
                self.nsem += 1
        assert b.sem[2] == sw, "buffer %s mixes SW and HW DMA" % b.name
        return b.sem

    def dma(self, q, out_b, out_ap, in_b, in_ap, **kw):
        owner = out_b if not out_b.dram else (in_b if not in_b.dram else out_b)
        st = self.bufsem(owner, sw=(q == "pool"))
        self.deps(q, [in_b], [out_b], waw=not out_b.dram, same_ok=False)
        st[1] += 16
        ev = (st[0], st[1])
        self.per[q].append(("dma", out_ap, in_ap, st[0], kw))
        self.mark(ev, [in_b], [out_b], append=out_b.dram)

    def cc(self, kind, op, in_b, in_ap, out_b, out_ap, append=False):
        self.deps("pool", [in_b], [out_b], same_ok=False)
        self.cc_n += 1
        ev = (self.cc_sem, self.cc_n)
        self.per["pool"].append(("cc", kind, op, in_ap, out_ap))
        self.mark(ev, [in_b], [out_b], append=append)

    def barrier(self):
        for en in self.ENG:
            for ev in list(self.allsems.values()):
                self.need(en, ev, same_ok=False)
        self.flush()

    def recycle(self, sems):
        for st in sems:
            (self.free_sw if st[2] else self.free_sems).append(st)

    def flush(self):
        block = self.block

        def mk(en):
            items = self.per[en]
            self.per[en] = []

            def body(h):
                for it in items:
                    if it[0] == "wait":
                        h.wait_ge(it[1], it[2])
                    elif it[0] == "op":
                        getattr(h, it[1])(**it[3]).then_inc(it[2], 1)
                    elif it[0] == "clear":
                        h.sem_clear(it[1])
                    elif it[0] == "dma":
                        h.dma_start(out=it[1], in_=it[2], **it[4]).then_inc(it[3], 16)
                    elif it[0] == "cc":
                        h.collective_compute(it[1], it[2], replica_groups=RG, ins=[it[3]], outs=[it[4]]).then_inc(self.cc_sem, 1)
            return body
        block.tensor(mk("pe"))
        block.scalar(mk("act"))
        block.vector(mk("dve"))
        block.gpsimd(mk("pool"))
        block.sync(mk("sp"))


def host_consts():
    c = {}
    n1 = np.arange(128)[:, None].astype(np.float64)
    k1 = np.arange(128)[None, :].astype(np.float64)
    a = 2 * np.pi * n1 * k1 / 128.0
    c["f128"] = np.concatenate([np.cos(a), -np.sin(a)], axis=1).astype(NPBF)
    n2 = (np.arange(2)[None, :, None] * 128 + np.arange(128)[:, None, None]).astype(np.float64)
    kk = np.arange(128)[None, None, :].astype(np.float64)
    a = 2 * np.pi * n2 * kk / NFFT
    c["tw"] = np.stack([np.cos(a), -np.sin(a)], axis=2).astype(np.float32)
    n2 = (np.arange(2)[None, :, None] * 128 + np.arange(128)[:, None, None]).astype(np.float64)
    k2 = np.arange(256)[None, None, :].astype(np.float64)
    a = 2 * np.pi * n2 * k2 / 256.0
    c["f256"] = np.stack([np.cos(a), -np.sin(a), np.sin(a)], axis=2).astype(NPBF)
    kk2 = (np.arange(2)[None, :, None] * 128 + np.arange(128)[:, None, None]).astype(np.float64)
    nn2 = np.arange(256)[None, None, :].astype(np.float64)
    a = 2 * np.pi * kk2 * nn2 / 256.0
    gr, gi = np.cos(a), np.sin(a)
    c["g256"] = np.stack([np.concatenate([gr, gi], axis=2), np.concatenate([-gi, gr], axis=2)], axis=2).astype(NPBF)
    kk1 = np.arange(128)[:, None].astype(np.float64)
    nn2 = np.arange(256)[None, :].astype(np.float64)
    a = 2 * np.pi * kk1 * nn2 / NFFT
    c["itw"] = np.stack([np.cos(a), np.sin(a)], axis=1).astype(np.float32)
    kk1 = np.arange(128)[:, None].astype(np.float64)
    nn1 = np.arange(64)[None, :].astype(np.float64)
    a = 2 * np.pi * kk1 * nn1 / 128.0
    c["g128"] = np.stack([np.cos(a) / NFFT, -np.sin(a) / NFFT], axis=1).astype(NPBF)
    P = np.zeros((128, 128), np.float32)
    for m in range(128):
        d = m % 64
        half = (d % 32) // 16
        if half == 0:
            P[m + 16, m] = -1.0
        else:
            P[m - 16, m] = 1.0
    c["prot"] = P.astype(NPBF)
    t = np.linspace(0.0, 1.0, L, dtype=np.float32)
    omega = ((2.0 * np.pi / L) * np.arange(L, dtype=np.float32))[:, None].astype(np.float32)
    bands = np.linspace(1e-4, 15, 16, dtype=np.float32)[None, :]
    ang = (omega * bands).astype(np.float32)
    emb = np.concatenate([t[:, None], np.cos(ang), -np.sin(ang)], axis=-1).astype(np.float32)
    pos2 = (L - np.arange(L)) % L
    embfull = np.concatenate([emb, emb[pos2]], axis=0)
    c["embT"] = np.ascontiguousarray(embfull.T).astype(np.float32)
    c["tpos"] = np.concatenate([t, t[pos2]]).astype(np.float32)
    return c


def rope_tables(tok0):
    n = tok0 + np.arange(NT)
    row = (n // 64).astype(np.float32)
    col = (n % 64).astype(np.float32)
    inv_freq = (10000.0 ** (-np.arange(0, 32, 2, dtype=np.float32) / 32.0)).astype(np.float32)
    cos = np.zeros((128, NT), np.float32)
    sin = np.zeros((128, NT), np.float32)
    for p in range(128):
        d = p % 64
        axis = d // 32
        f = d % 16
        pos = row if axis == 0 else col
        ang = (pos * inv_freq[f]).astype(np.float32)
        cos[p] = np.cos(ang)
        sin[p] = np.sin(ang)
    return cos, sin


def decay_table(r):
    t = np.linspace(0.0, 1.0, L, dtype=np.float32)
    min_decay = math.log(1e-2) / 1.5
    max_decay = math.log(1e-2) / 0.3
    deltas = np.abs(np.linspace(min_decay, max_decay, 512, dtype=np.float32))[r * 128:(r + 1) * 128]
    pos2 = (L - np.arange(L)) % L
    tfull = np.concatenate([t, t[pos2]])
    dec = np.exp(-tfull[None, :] * deltas[:, None]).astype(np.float32)
    dec[:, L] = 0.0
    return dec


def build(cfg):
    dbg = cfg.get("debug", False)
    stop_after = cfg.get("stop_after", "all")
    n_lat_tiles = cfg.get("n_lat_tiles", 8)
    n_grp = cfg.get("n_grp", 32)
    n_heads = cfg.get("n_heads", 4)
    n_qt = cfg.get("n_qt", 8)
    nc = bass.Bass("TRN2", target_bir_lowering=False)

    def din(name, shape, dt=F32):
        return Buf(nc.dram_tensor(name, list(shape), dt, kind="ExternalInput").ap(), name, dram=True)

    def dout(name, shape, dt=F32):
        return Buf(nc.dram_tensor(name, list(shape), dt, kind="ExternalOutput").ap(), name, dram=True)

    def dint(name, shape, dt):
        return Buf(nc.dram_tensor(name, list(shape), dt).ap(), name, dram=True)

    xT = din("xT", [D, NAUG])
    c2 = din("c2", [128, 8, 2])
    ada_w = din("ada_w", [D, 9 * D])
    ada_bT = din("ada_bT", [128, 72])
    normg = din("normg", [128, 3, 8])
    finalg = din("finalg", [128, 8])
    wg = [din("wg%d" % i, [D, FH]) for i in range(2)]
    wu = [din("wu%d" % i, [D, FH]) for i in range(2)]
    wd = [din("wd%d" % i, [FH, D]) for i in range(2)]
    w_in = din("w_in", [D, 3072])
    w_out = din("w_out", [D, D])
    convw = din("convw", [128, 12, 3])
    convb = din("convb", [128, 12])
    maskLR = din("maskLR", [128, 2])
    tmask = din("tmask", [128, 4])
    ropec = din("ropec", [128, NT])
    ropes = din("ropes", [128, NT])
    prot_d = din("prot", [128, 128], BF16)
    embT = din("embT", [33, NFFT])
    decay = din("decay", [128, NFFT])
    fw1 = din("fw1", [33, 64])
    fvec = din("fvec", [64, 4])
    fwi = din("fwi", [64, 2, 64])
    fwo = din("fwo", [64, 256])
    hbias = din("hbias", [128, 4])
    dlam = din("dlam", [128, 256])
    subg = din("subg", [128, 1])
    f128_d = din("f128", [128, 256], BF16)
    tw_d = din("tw", [128, 2, 2, 128])
    f256_d = din("f256", [128, 2, 3, 256], BF16)
    g256_d = din("g256", [128, 2, 2, 512], BF16)
    itw_d = din("itw", [128, 2, 256])
    g128_d = din("g128", [128, 2, 64], BF16)
    outT = dout("outT", [D, NT])
    h1T = dint("h1T", [D, NAUG], F32)
    x0T = dint("x0T", [512, NT], BF16)
    uvT = dint("uvT", [512, NT], BF16)
    qT = dint("qT", [512, NT], BF16)
    kcT = dint("kcT", [512, CTX], BF16)
    vcx = dint("vcx", [CTX, 512], BF16)
    ydT = dint("ydT", [512, NT], BF16)
    kin = dint("kin", [512, NT], BF16)
    vin = dint("vin", [4 * NT, 128], BF16)
    kall = dint("kall", [4 * 512, NT], BF16)
    vall = dint("vall", [4 * 4 * NT, 128], BF16)
    rsin1 = dint("rsin1", [512, L], BF16)
    uvch = dint("uvch", [128, L], BF16)
    kscr = dint("kscr", [128, NFFT], F32)
    rsin2 = dint("rsin2", [4 * 512, NT], BF16)
    yall = dint("yall", [512, NT], BF16)

    es = ExitStack()
    with es:
        P = Prog(nc, es)
        P.strict = bool(cfg.get("strict"))
        uid = [0]

        def sb(name, shape, dt):
            return Buf(es.enter_context(nc.sbuf_tensor(name, list(shape), dt)), name)

        block = es.enter_context(nc.Block())
        P.block = block

        def PE(out_b, out_ap, lhs_b, lhsT, rhs_b, rhs, start, stop):
            P.op("pe", "matmul", reads=[lhs_b, rhs_b], writes=[out_b], out=out_ap, lhsT=lhsT, rhs=rhs, start=start, stop=stop)

        def ACT(out_b, out_ap, in_b, in_ap, func, extra=(), **kw):
            P.op("act", "activation", reads=[in_b] + list(extra), writes=[out_b], out=out_ap, in_=in_ap, func=func, **kw)

        def TT(en, out_b, out_ap, a_b, a_ap, b_b, b_ap, op):
            P.op(en, "tensor_tensor", reads=[a_b, b_b], writes=[out_b], out=out_ap, in0=a_ap, in1=b_ap, op=op)

        def TS(en, out_b, out_ap, a_b, a_ap, s1, s2, op0, op1=None, extra=()):
            kw = dict(out=out_ap, in0=a_ap, scalar1=s1, scalar2=s2, op0=op0)
            if op1 is not None:
                kw["op1"] = op1
            P.op(en, "tensor_scalar", reads=[a_b] + list(extra), writes=[out_b], **kw)

        def STT(out_b, out_ap, a_b, a_ap, sc_b, sc_ap, b_b, b_ap, op0, op1):
            rd = [a_b, b_b] + ([sc_b] if sc_b is not None else [])
            P.op("dve", "scalar_tensor_tensor", reads=rd, writes=[out_b], out=out_ap, in0=a_ap, scalar=sc_ap, in1=b_ap, op0=op0, op1=op1)

        P.dummy = sb("dummy_t", [128, 1], F32)
        ones1024 = sb("ones1024", [128, 128], BF16)
        ones128 = sb("ones128", [128, 128], BF16)
        ones1 = sb("ones1", [128, 128], BF16)
        P.op("pool", "memset", writes=[ones1024], ap=ones1024[:], constant=1.0 / 1024)
        P.op("pool", "memset", writes=[ones128], ap=ones128[:], constant=1.0 / 128)
        P.op("pool", "memset", writes=[ones1], ap=ones1[:], constant=1.0)
        cst = sb("cst", [128, 4], F32)
        P.op("pool", "memset", writes=[cst], ap=cst[:, 0:1], constant=1e-6)
        P.op("pool", "memset", writes=[cst], ap=cst[:, 1:2], constant=1e-5)
        P.op("pool", "memset", writes=[cst], ap=cst[:, 2:3], constant=-PI)
        P.op("pool", "memset", writes=[cst], ap=cst[:, 3:4], constant=0.0)
        EPS = {1e-6: cst[:, 0:1], 1e-5: cst[:, 1:2]}
        md = sb("md", [128, 72, 2], F32)
        ng = sb("ng", [128, 3, 8], F32)
        fg = sb("fg", [128, 8], F32)
        P.dma("sp", ng, ng[:], normg, normg[:])
        P.dma("sp", fg, fg[:], finalg, finalg[:])

        pq = [Buf(es.enter_context(nc.psum_tensor("pq%d" % i, [128, 1024], F32)), "pq%d" % i) for i in range(4)]
        pb = []
        for i in range(8):
            pb.append(Buf(pq[i // 2].t[:, (i % 2) * 512:(i % 2 + 1) * 512], "pb%d" % i))

        def rsqrt_op(dst, src, T, eps, np_=128):
            ACT(dst, dst[:np_, :T], src, src[:np_, :T], AF.Sqrt, extra=[cst], bias=EPS[eps][:np_, :], scale=1.0)
            P.op("dve", "reciprocal", reads=[dst], writes=[dst], out=dst[:np_, :T], in_=dst[:np_, :T])

        class Local:
            def __init__(self):
                self.les = ExitStack()
                self.bufs = []

            def sb(self, name, shape, dt):
                uid[0] += 1
                b = Buf(self.les.enter_context(nc.sbuf_tensor("%s_u%d" % (name, uid[0]), list(shape), dt)), name)
                self.bufs.append(b)
                return b

            def close(self):
                P.barrier()
                rs = []
                for b in self.bufs:
                    if b.sem is not None:
                        rs.append(b.sem)
                        b.sem = None
                P.recycle(rs)
                self.les.close()

        def phase_mod():
            lc = Local()
            ct = lc.sb("ct", [128, 8, 2], F32)
            sc = lc.sb("sc", [128, 8, 2], F32)
            ab = lc.sb("ab", [128, 72], F32)
            aw = [lc.sb("aw%d" % i, [128, 8, 1024], F32) for i in range(2)]
            P.dma("sp", ct, ct[:], c2, c2[:])
            P.dma("sp", ab, ab[:], ada_bT, ada_bT[:])
            ACT(sc, sc[:], ct, ct[:], AF.Silu)
            pm = pb[0]
            awv = ada_w.t.rearrange("(kc p) n -> p kc n", p=128)
            for i in range(9):
                a = aw[i % 2]
                P.dma("sp", a, a[:, :, :], ada_w, awv[:, :, i * 1024:(i + 1) * 1024])
                for fc in range(8):
                    col = (i * 8 + fc) * 2
                    for kc in range(8):
                        PE(pm, pm[:, col:col + 2], a, a[:, kc, fc * 128:(fc + 1) * 128], sc, sc[:, kc, :], kc == 0, kc == 7)
            TT("dve", md, md[:], pm, pm[:, 0:144].rearrange("p (a b) -> p a b", b=2), ab, ab[:].unsqueeze(2).broadcast_to([128, 72, 2]), ALU.add)
            for j in range(3):
                sl = md[:, (3 * j + 1) * 8:(3 * j + 2) * 8, :]
                STT(md, sl, md, sl, None, 1.0, ng, ng[:, j, :].unsqueeze(2).broadcast_to([128, 8, 2]), ALU.add, ALU.mult)
            for i in (2, 8):
                sl = md[:, i * 8:(i + 1) * 8, :]
                TS("dve", md, sl, md, sl, 0.5, None, ALU.mult)
            lc.close()

        def adaln(X, U, T, j, mcol, sq, rstd, tmp, pms, ucol0=0):
            ACT(sq, sq[:, 0:8, :T], X, X[:, :, :T], AF.Square)
            for fc in range(8):
                PE(pms, pms[:, :T], ones1024, ones1024[:], sq, sq[:, fc, :T], fc == 0, fc == 7)
            rsqrt_op(rstd, pms, T, 1e-6)
            for fc in range(8):
                tt = tmp[fc % 2]
                STT(tt, tt[:, :T], X, X[:, fc, :T], md, md[:, (3 * j + 1) * 8 + fc, mcol:mcol + 1], rstd, rstd[:, :T], ALU.mult, ALU.mult)
                ACT(U, U[:, fc, ucol0:ucol0 + T], tt, tt[:, :T], AF.Identity, extra=[md], bias=md[:, (3 * j) * 8 + fc, mcol:mcol + 1], scale=1.0)

        FFN_BLK = [(0, 8), (8, 8), (16, 6)]

        def phase_ffn(which, tiles, j, final):
            lc = Local()
            Wg = [lc.sb("Wg%d" % b, [128, 8, n * 128], BF16) for b, (s, n) in enumerate(FFN_BLK)]
            Wu = [lc.sb("Wu%d" % b, [128, 8, n * 128], BF16) for b, (s, n) in enumerate(FFN_BLK)]
            Wd = [lc.sb("Wd%d" % b, [128, n, 1024], BF16) for b, (s, n) in enumerate(FFN_BLK)]
            X = lc.sb("X", [128, 8, 512], F32)
            U = lc.sb("U", [128, 8, 512], BF16)
            act = lc.sb("act", [128, NJ, 512], BF16)
            sg = [lc.sb("sg%d" % i, [128, 512], F32) for i in range(2)]
            rstd = lc.sb("rstd", [128, 512], F32)
            tmp = [lc.sb("tmp%d" % i, [128, 512], F32) for i in range(2)]
            wgv = wg[which].t.rearrange("(kc p) n -> p kc n", p=128)
            wuv = wu[which].t.rearrange("(kc p) n -> p kc n", p=128)
            wdv = wd[which].t.rearrange("(j p) n -> p j n", p=128)
            for b, (s, n) in enumerate(FFN_BLK):
                P.dma("pool", Wg[b], Wg[b][:, :, :], wg[which], wgv[:, :, s * 128:(s + n) * 128])
                P.dma("pool", Wu[b], Wu[b][:, :, :], wu[which], wuv[:, :, s * 128:(s + n) * 128])
            for b, (s, n) in enumerate(FFN_BLK):
                P.dma("pool", Wd[b], Wd[b][:, :, :], wd[which], wdv[:, s:s + n, :])
            src = xT if which == 0 else h1T
            srcv = src.t.rearrange("(fc p) n -> p fc n", p=128)
            dstv = h1T.t.rearrange("(fc p) n -> p fc n", p=128)
            outv = outT.t.rearrange("(fc p) n -> p fc n", p=128)
            pms = pb[6]
            for (c0, T, mcol) in tiles:
                P.dma("sp", X, X[:, :, :T], src, srcv[:, :, c0:c0 + T])
                adaln(X, U, T, j, mcol, act, rstd, tmp, pms)
                for jc in range(NJ):
                    b = 0 if jc < 8 else (1 if jc < 16 else 2)
                    jl = jc - FFN_BLK[b][0]
                    pg = pb[jc % 2]
                    pu = pb[2 + jc % 2]
                    for kc in range(8):
                        PE(pg, pg[:, :T], Wg[b], Wg[b][:, kc, jl * 128:(jl + 1) * 128], U, U[:, kc, :T], kc == 0, kc == 7)
                    for kc in range(8):
                        PE(pu, pu[:, :T], Wu[b], Wu[b][:, kc, jl * 128:(jl + 1) * 128], U, U[:, kc, :T], kc == 0, kc == 7)
                    s_ = sg[jc % 2]
                    ACT(s_, s_[:, :T], pg, pg[:, :T], AF.Silu)
                    TT("dve", act, act[:, jc, :T], pu, pu[:, :T], s_, s_[:, :T], ALU.mult)
                for m in range(8):
                    pd = pb[4 + m % 2]
                    for jc in range(NJ):
                        b = 0 if jc < 8 else (1 if jc < 16 else 2)
                        jl = jc - FFN_BLK[b][0]
                        PE(pd, pd[:, :T], Wd[b], Wd[b][:, jl, m * 128:(m + 1) * 128], act, act[:, jc, :T], jc == 0, jc == NJ - 1)
                    STT(X, X[:, m, :T], pd, pd[:, :T], md, md[:, (3 * j + 2) * 8 + m, mcol:mcol + 1], X, X[:, m, :T], ALU.mult, ALU.add)
                if not final:
                    P.dma("sp", h1T, dstv[:, :, c0:c0 + T], X, X[:, :, :T])
                else:
                    ACT(act, act[:, 0:8, :T], X, X[:, :, :T], AF.Square)
                    for fc in range(8):
                        PE(pms, pms[:, :T], ones1024, ones1024[:], act, act[:, fc, :T], fc == 0, fc == 7)
                    rsqrt_op(rstd, pms, T, 1e-6)
                    for fc in range(8):
                        STT(X, X[:, fc, :T], X, X[:, fc, :T], fg, fg[:, fc:fc + 1], rstd, rstd[:, :T], ALU.mult, ALU.mult)
                    P.dma("sp", outT, outv[:, :, c0 - C_LAT:c0 - C_LAT + T], X, X[:, :, :T])
            lc.close()

        lat_tiles = [(C_LAT + 512 * i, 512, 0) for i in range(n_lat_tiles)]
        all_tiles = lat_tiles + [(0, 2, 0), (C_HR, 2, 0), (C_CTX, CTX, 1)]

        def phase_inproj():
            lc = Local()
            l1 = Local()
            U2 = lc.sb("U2", [128, 8, NAUG], BF16)
            Win = [lc.sb("Win%d" % i, [128, 8, 1024], BF16) for i in range(3)]
            winv = w_in.t.rearrange("(kc p) n -> p kc n", p=128)
            for i in range(3):
                P.dma("pool", Win[i], Win[i][:, :, :], w_in, winv[:, :, i * 1024:(i + 1) * 1024])
            cw = lc.sb("cw", [128, 12, 3], F32)
            cb = lc.sb("cb", [128, 12], F32)
            mlr = lc.sb("mlr", [128, 2], F32)
            tmk = lc.sb("tmk", [128, 4], F32)
            prot = lc.sb("prot", [128, 128], BF16)
            X = l1.sb("X2", [128, 8, 512], F32)
            sq = l1.sb("sq2", [128, 8, 512], BF16)
            rstd = l1.sb("rstd2", [128, 512], F32)
            tmp = [l1.sb("tmp2_%d" % i, [128, 512], F32) for i in range(2)]
            P.dma("sp", cw, cw[:], convw, convw[:])
            P.dma("sp", cb, cb[:], convb, convb[:])
            P.dma("sp", mlr, mlr[:], maskLR, maskLR[:])
            P.dma("sp", tmk, tmk[:], tmask, tmask[:])
            P.dma("sp", prot, prot[:], prot_d, prot_d[:])
            srcv = h1T.t.rearrange("(fc p) n -> p fc n", p=128)
            if n_lat_tiles < 8:
                P.op("pool", "memset", writes=[U2], ap=U2[:], constant=0.0)
            for (c0, T, mcol) in all_tiles:
                P.dma("sp", X, X[:, :, :T], h1T, srcv[:, :, c0:c0 + T])
                adaln(X, U2, T, 1, mcol, sq, rstd, tmp, pb[6], ucol0=c0)
            TS("pool", U2, U2[:, :, 0:2], U2, U2[:, :, 0:2], mlr[:, 0:1], None, ALU.mult, extra=[mlr])
            TS("pool", U2, U2[:, :, C_HR:C_HR + 2], U2, U2[:, :, C_HR:C_HR + 2], mlr[:, 1:2], None, ALU.mult, extra=[mlr])

            def wcol(ch):
                c = ch * 128
                return Win[c // 1024], c % 1024

            if cfg.get("ip") == "a":
                l1.close()
                lc.close()
                return

            tA = [l1.sb("tA%d" % i, [128, 512], F32) for i in range(2)]
            cx1 = l1.sb("cx1", [128, 512], F32)
            cv = l1.sb("cv", [128, 512], F32)
            x0w = [l1.sb("x0w%d" % i, [128, 4, 512], BF16) for i in range(2)]
            uvw = [l1.sb("uvw%d" % i, [128, 4, 512], BF16) for i in range(2)]
            um = [l1.sb("um%d" % i, [128, 4, 512], BF16) for i in range(4)]
            x0v = x0T.t.rearrange("(i p) t -> p i t", p=128)
            uvv = uvT.t.rearrange("(i p) t -> p i t", p=128)
            rs1v = rsin1.t.rearrange("(i p) t -> p i t", p=128)
            nwin = (n_lat_tiles * 512 + 509) // 510
            cnt = 0
            for w in range(nwin):
                t0 = 510 * w
                n = min(510, n_lat_tiles * 512 - t0)
                c0 = 1 + t0
                xw = x0w[w % 2]
                uw = uvw[w % 2]
                for i in range(4):
                    for ch, kind in ((4 + i, 1), (8 + i, 2), (i, 0)):
                        pz = pb[cnt % 4]
                        ta = tA[cnt % 2]
                        cnt += 1
                        wb_, off = wcol(ch)
                        for kc in range(8):
                            PE(pz, pz[:, :n + 2], wb_, wb_[:, kc, off:off + 128], U2, U2[:, kc, c0:c0 + n + 2], kc == 0, kc == 7)
                        ACT(ta, ta[:, :n], pz, pz[:, 1:n + 1], AF.Identity, extra=[cw, cb], scale=cw[:, ch, 1:2], bias=cb[:, ch:ch + 1])
                        STT(ta, ta[:, :n], pz, pz[:, 0:n], cw, cw[:, ch, 0:1], ta, ta[:, :n], ALU.mult, ALU.add)
                        if kind == 0:
                            db, dap = xw, xw[:, i, :n]
                        elif kind == 1:
                            db, dap = cx1, cx1[:, :n]
                        else:
                            db, dap = cv, cv[:, :n]
                        STT(db, dap, pz, pz[:, 2:n + 2], cw, cw[:, ch, 2:3], ta, ta[:, :n], ALU.mult, ALU.add)
                    TT("pool", uw, uw[:, i, :n], cx1, cx1[:, :n], cv, cv[:, :n], ALU.mult)
                P.dma("sp", x0T, x0v[:, :, t0:t0 + n], xw, xw[:, :, :n])
                P.dma("sp", uvT, uvv[:, :, t0:t0 + n], uw, uw[:, :, :n])
                for jb in range(4):
                    u_ = um[jb]
                    TS("pool", u_, u_[:, :, :n], uw, uw[:, :, :n], tmk[:, jb:jb + 1], None, ALU.mult, extra=[tmk])
                    P.dma("sp", rsin1, rs1v[:, :, jb * NT + t0:jb * NT + t0 + n], u_, u_[:, :, :n])

            l1.close()
            if cfg.get("ip") == "b":
                lc.close()
                return
            l2 = Local()
            cosb = l2.sb("cosb", [128, 512], F32)
            sinb = l2.sb("sinb", [128, 512], F32)
            qb = [l2.sb("qb%d" % i, [128, 512], BF16) for i in range(2)]
            t1 = [l2.sb("t1_%d" % i, [128, 512], F32) for i in range(2)]
            t2 = [l2.sb("t2_%d" % i, [128, 512], F32) for i in range(2)]
            qo = [l2.sb("qo%d" % i, [128, 512], BF16) for i in range(2)]
            vt = [l2.sb("vt%d" % i, [128, 512], BF16) for i in range(2)]
            cnt = 0
            for it in range(n_lat_tiles):
                c0 = C_LAT + 512 * it
                if not cfg.get("noload"):
                    P.dma("sp", cosb, cosb[:], ropec, ropec[:, it * 512:(it + 1) * 512])
                    P.dma("sp", sinb, sinb[:], ropes, ropes[:, it * 512:(it + 1) * 512])
                for ch in range(cfg.get("ch0", 12), cfg.get("ch1", 20)):
                    hh = (ch - 12) % 4
                    pqk = pb[cnt % 2]
                    pr = pb[2 + cnt % 2]
                    q_ = qb[cnt % 2]
                    a1, a2, o_ = t1[cnt % 2], t2[cnt % 2], qo[cnt % 2]
                    cnt += 1
                    wb_, off = wcol(ch)
                    for kc in range(8):
                        PE(pqk, pqk[:, :], wb_, wb_[:, kc, off:off + 128], U2, U2[:, kc, c0:c0 + 512], kc == 0, kc == 7)
                    ACT(q_, q_[:], pqk, pqk[:, :], AF.Identity, extra=[cst], bias=cst[:, 3:4], scale=1.0)
                    if not cfg.get("noprot"):
                        PE(pr, pr[:, :], prot, prot[:], q_, q_[:], True, True)
                    P.op("dve", "tensor_tensor", reads=[pqk, cosb, q_], writes=[a1], out=a1[:], in0=pqk[:, :], in1=cosb[:], op=ALU.mult)
                    if not cfg.get("noprot"):
                        TT("dve", a2, a2[:], pr, pr[:, :], sinb, sinb[:], ALU.mult)
                    else:
                        TT("dve", a2, a2[:], pqk, pqk[:, :], sinb, sinb[:], ALU.mult)
                    TT("dve", o_, o_[:], a1, a1[:], a2, a2[:], ALU.add)
                    dst = qT if ch < 16 else kin
                    if not cfg.get("nostore"):
                        P.dma("sp", dst, dst[hh * 128:(hh + 1) * 128, it * 512:(it + 1) * 512], o_, o_[:])
                for s4 in range(0 if not cfg.get("skipv") else 4, 4):
                    pv = pb[4 + s4 % 2]
                    v_ = vt[s4 % 2]
                    for kc in range(8):
                        PE(pv, pv[:, :], U2, U2[:, kc, c0 + s4 * 128:c0 + (s4 + 1) * 128], Win[2], Win[2][:, kc, 512:1024], kc == 0, kc == 7)
                    ACT(v_, v_[:], pv, pv[:, :], AF.Identity, extra=[cst], bias=cst[:, 3:4], scale=1.0)
                    P.dma("sp", vin, vin.t.rearrange("(h t) e -> t h e", h=4)[it * 512 + s4 * 128:it * 512 + (s4 + 1) * 128, :, :],
                          v_, v_[:].rearrange("p (h e) -> p h e", h=4))
            if cfg.get("skipctx"):
                l2.close()
                lc.close()
                return
            for ch in range(16, 20):
                hh = ch - 16
                pk = pb[cnt % 2]
                o_ = qo[cnt % 2]
                cnt += 1
                wb_, off = wcol(ch)
                for kc in range(8):
                    PE(pk, pk[:, :CTX], wb_, wb_[:, kc, off:off + 128], U2, U2[:, kc, C_CTX:C_CTX + CTX], kc == 0, kc == 7)
                ACT(o_, o_[:, :CTX], pk, pk[:, :CTX], AF.Identity, extra=[cst], bias=cst[:, 3:4], scale=1.0)
                P.dma("sp", kcT, kcT[hh * 128:(hh + 1) * 128, :], o_, o_[:, :CTX])
            for s2 in range(2):
                pv = pb[4 + s2 % 2]
                v_ = vt[s2 % 2]
                for kc in range(8):
                    PE(pv, pv[:, :], U2, U2[:, kc, C_CTX + s2 * 128:C_CTX + (s2 + 1) * 128], Win[2], Win[2][:, kc, 512:1024], kc == 0, kc == 7)
                ACT(v_, v_[:], pv, pv[:, :], AF.Identity, extra=[cst], bias=cst[:, 3:4], scale=1.0)
                P.dma("sp", vcx, vcx[s2 * 128:(s2 + 1) * 128, :], v_, v_[:])
            l2.close()
            lc.close()

        def phase_filter():
            lc = Local()
            w1 = lc.sb("w1", [33, 64], F32)
            fv = lc.sb("fv", [64, 4], F32)
            fb = lc.sb("fb", [64, 3], F32)
            wi32 = lc.sb("wi32", [64, 2, 64], F32)
            wi = lc.sb("wi", [64, 2, 64], BF16)
            wo32 = lc.sb("wo32", [64, 256], F32)
            wo = lc.sb("wo", [64, 256], BF16)
            P.dma("sp", w1, w1[:], fw1, fw1[:])
            P.dma("sp", fv, fv[:], fvec, fvec[:])
            P.dma("sp", wi32, wi32[:], fwi, fwi[:])
            P.dma("sp", wo32, wo32[:], fwo, fwo[:])
            P.op("dve", "tensor_copy", reads=[wi32], writes=[wi], out=wi[:], in_=wi32[:])
            P.op("dve", "tensor_copy", reads=[wo32], writes=[wo], out=wo[:], in_=wo32[:])
            TS("dve", fb, fb[:, 0:1], fv, fv[:, 0:1], fv[:, 1:2], None, ALU.mult)
            TS("dve", fb, fb[:, 1:3], fv, fv[:, 2:4], fv[:, 1:2], None, ALU.mult)
            em = [lc.sb("em%d" % i, [33, 512], F32) for i in range(2)]
            dc = [lc.sb("dc%d" % i, [128, 512], F32) for i in range(2)]
            aa = [lc.sb("aa%d" % i, [64, 512], F32) for i in range(2)]
            kk_ = [lc.sb("kk%d" % i, [64, 512], F32) for i in range(2)]
            hh_ = [lc.sb("hh%d" % i, [64, 512], F32) for i in range(3)]
            ko = [lc.sb("ko%d" % i, [128, 512], F32) for i in range(2)]
            MAGIC = 12582912.0
            cnt = 0
            ntile = cfg.get("n_ftile", 64)
            for tc_ in range(ntile):
                e_ = em[tc_ % 2]
                d_ = dc[tc_ % 2]
                P.dma("sp", e_, e_[:], embT, embT[:, tc_ * 512:(tc_ + 1) * 512])
                P.dma("sp", d_, d_[:], decay, decay[:, tc_ * 512:(tc_ + 1) * 512])
                hprev = None
                for layer in range(3):
                    pp = pb[cnt % 4]
                    a_ = aa[cnt % 2]
                    k_ = kk_[cnt % 2]
                    cnt += 1
                    if layer == 0:
                        PE(pp, pp[:64, :], w1, w1[:], e_, e_[:], True, True)
                    else:
                        PE(pp, pp[:64, :], wi32, wi32[:, layer - 1, :], hprev, hprev[:], True, True)
                    TS("dve", a_, a_[:], pp, pp[:64, :], fv[:, 1:2], fb[:, layer:layer + 1], ALU.mult, ALU.add, extra=[fv, fb])
                    TS("dve", k_, k_[:], a_, a_[:], 1.0 / (2 * PI), MAGIC, ALU.mult, ALU.add)
                    TS("dve", k_, k_[:], k_, k_[:], -MAGIC, -2 * PI, ALU.add, ALU.mult)
                    TT("dve", a_, a_[:], a_, a_[:], k_, k_[:], ALU.add)
                    hn = hh_[layer]
                    ACT(hn, hn[:], a_, a_[:], AF.Sin)
                    hprev = hn
                pk = pb[4 + tc_ % 2]
                dirc = 0 if tc_ < 32 else 1
                PE(pk, pk[:, :], wo32, wo32[:, dirc * 128:(dirc + 1) * 128], hprev, hprev[:], True, True)
                o_ = ko[tc_ % 2]
                TT("dve", o_, o_[:], pk, pk[:, :], d_, d_[:], ALU.mult)
                P.dma("sp", kscr, kscr[:, tc_ * 512:(tc_ + 1) * 512], o_, o_[:])
            lc.close()

        def phase_fft():
            lc = Local()
            f128 = lc.sb("f128", [128, 256], BF16)
            tw = lc.sb("tw", [128, 2, 2, 128], F32)
            f256 = lc.sb("f256", [128, 2, 3, 256], BF16)
            g256 = lc.sb("g256", [128, 2, 2, 512], BF16)
            itw = lc.sb("itw", [128, 2, 256], F32)
            g128 = lc.sb("g128", [128, 2, 64], BF16)
            gmk = lc.sb("gmk", [128, 4], F32)
            for sbuf_, d_ in ((f128, f128_d), (tw, tw_d), (f256, f256_d), (g256, g256_d), (itw, itw_d), (g128, g128_d), (gmk, tmask)):
                P.dma("sp", sbuf_, sbuf_[:], d_, d_[:])
            kx = [lc.sb("kx%d" % i, [128, 4, 256], BF16) for i in range(2)]
            xx = [lc.sb("xx%d" % i, [64, 4, 256], BF16) for i in range(2)]
            Bb = lc.sb("Bb", [128, 2, 2, 4, 128], BF16)
            Kr = lc.sb("Kr", [128, 1024], F32)
            Ki = lc.sb("Ki", [128, 1024], F32)
            Yb = lc.sb("Yb", [128, 2, 2, 4, 128], BF16)
            Db = lc.sb("Db", [128, 2, 4, 256], BF16)
            w1_ = lc.sb("fw_1", [128, 1024], F32)
            w2_ = lc.sb("fw_2", [128, 1024], F32)
            w3_ = lc.sb("fw_3", [128, 1024], F32)
            w4_ = lc.sb("fw_4", [128, 1024], F32)
            ym = [lc.sb("ym%d" % i, [64, 4, 256], BF16) for i in range(8)]
            kv = kscr.t.rearrange("c (a b) -> a c b", b=256)
            xv = uvch.t.rearrange("c (a b) -> a c b", b=256)

            def fwd(src, np_):
                for h in range(2):
                    for c in range(4):
                        PE(pq[h], pq[h][:, c * 256:(c + 1) * 256], src, src[:np_, c, h * 128:(h + 1) * 128], f128, f128[:np_, :], True, True)
                for h in range(2):
                    v = pq[h].t[:, :].rearrange("p (c r k) -> p c r k", c=4, r=2)
                    Ar, Ai = v[:, :, 0, :], v[:, :, 1, :]
                    Tr = tw[:, h, 0, :].unsqueeze(1).broadcast_to([128, 4, 128])
                    Ti = tw[:, h, 1, :].unsqueeze(1).broadcast_to([128, 4, 128])
                    a1 = w1_[:, 0:512].rearrange("p (c k) -> p c k", c=4)
                    a2 = w2_[:, 0:512].rearrange("p (c k) -> p c k", c=4)
                    a3 = w3_[:, 0:512].rearrange("p (c k) -> p c k", c=4)
                    a4 = w4_[:, 0:512].rearrange("p (c k) -> p c k", c=4)
                    TT("dve", w1_, a1, pq[h], Ar, tw, Tr, ALU.mult)
                    TT("dve", w2_, a2, pq[h], Ai, tw, Ti, ALU.mult)
                    TT("dve", w3_, a3, pq[h], Ar, tw, Ti, ALU.mult)
                    TT("dve", w4_, a4, pq[h], Ai, tw, Tr, ALU.mult)
                    TT("dve", Bb, Bb[:, h, 0, :, :], w1_, a1, w2_, a2, ALU.subtract)
                    TT("dve", Bb, Bb[:, h, 1, :, :], w3_, a3, w4_, a4, ALU.add)
                for o in range(2):
                    osl = slice(o * 128, (o + 1) * 128)
                    seq = [(0, 0), (2, 1)]
                    n = 0
                    for h in range(2):
                        for var, part in seq:
                            PE(pq[2], pq[2][:, o * 512:(o + 1) * 512], f256, f256[:, h, var, osl], Bb,
                               Bb[:, h, part, :, :].rearrange("p c k -> p (c k)"), n == 0, n == 3)
                            n += 1
                    seq = [(1, 0), (0, 1)]
                    n = 0
                    for h in range(2):
                        for var, part in seq:
                            PE(pq[3], pq[3][:, o * 512:(o + 1) * 512], f256, f256[:, h, var, osl], Bb,
                               Bb[:, h, part, :, :].rearrange("p c k -> p (c k)"), n == 0, n == 3)
                            n += 1

            for g in range(n_grp):
                cc0 = g * 4
                kx_ = kx[g % 2]
                xx_ = xx[g % 2]
                P.dma("pool", kx_, kx_[:, :, :], kscr, kv[:, cc0:cc0 + 4, :])
                P.dma("sp", xx_, xx_[:, :, :], uvch, xv[:, cc0:cc0 + 4, :])
                fwd(kx_, 128)
                ACT(Kr, Kr[:], pq[2], pq[2][:, :], AF.Identity, extra=[cst], bias=cst[:, 3:4], scale=1.0)
                ACT(Ki, Ki[:], pq[3], pq[3][:, :], AF.Identity, extra=[cst], bias=cst[:, 3:4], scale=1.0)
                fwd(xx_, 64)
                TT("dve", w1_, w1_[:], pq[2], pq[2][:, :], Kr, Kr[:], ALU.mult)
                TT("dve", w2_, w2_[:], pq[3], pq[3][:, :], Ki, Ki[:], ALU.mult)
                TT("dve", w3_, w3_[:], pq[2], pq[2][:, :], Ki, Ki[:], ALU.mult)
                TT("dve", w4_, w4_[:], pq[3], pq[3][:, :], Kr, Kr[:], ALU.mult)
                TT("dve", Yb, Yb[:, 0, :, :, :].rearrange("p o c k -> p (o c k)"), w1_, w1_[:], w2_, w2_[:], ALU.subtract)
                TT("dve", Yb, Yb[:, 1, :, :, :].rearrange("p o c k -> p (o c k)"), w3_, w3_[:], w4_, w4_[:], ALU.add)
                for c in range(4):
                    dst = pq[c // 2]
                    dap = dst[:, (c % 2) * 512:(c % 2 + 1) * 512]
                    n = 0
                    for o in range(2):
                        for part in range(2):
                            PE(dst, dap, Yb, Yb[:, part, o, c, :], g256, g256[:, o, part, :], n == 0, n == 3)
                            n += 1
                for cp in range(2):
                    v = pq[cp].t[:, :].rearrange("p (c r n) -> p c r n", c=2, r=2)
                    Cr, Ci = v[:, :, 0, :], v[:, :, 1, :]
                    Tr = itw[:, 0, :].unsqueeze(1).broadcast_to([128, 2, 256])
                    Ti = itw[:, 1, :].unsqueeze(1).broadcast_to([128, 2, 256])
                    a1 = w1_[:, 0:512].rearrange("p (c k) -> p c k", c=2)
                    a2 = w2_[:, 0:512].rearrange("p (c k) -> p c k", c=2)
                    a3 = w3_[:, 0:512].rearrange("p (c k) -> p c k", c=2)
                    a4 = w4_[:, 0:512].rearrange("p (c k) -> p c k", c=2)
                    TT("dve", w1_, a1, pq[cp], Cr, itw, Tr, ALU.mult)
                    TT("dve", w2_, a2, pq[cp], Ci, itw, Ti, ALU.mult)
                    TT("dve", w3_, a3, pq[cp], Cr, itw, Ti, ALU.mult)
                    TT("dve", w4_, a4, pq[cp], Ci, itw, Tr, ALU.mult)
                    TT("dve", Db, Db[:, 0, 2 * cp:2 * cp + 2, :], w1_, a1, w2_, a2, ALU.subtract)
                    TT("dve", Db, Db[:, 1, 2 * cp:2 * cp + 2, :], w3_, a3, w4_, a4, ALU.add)
                for cp in range(2):
                    dap = pq[2][:64, cp * 512:(cp + 1) * 512]
                    PE(pq[2], dap, g128, g128[:, 0, :], Db, Db[:, 0, 2 * cp:2 * cp + 2, :].rearrange("p c n -> p (c n)"), True, False)
                    PE(pq[2], dap, g128, g128[:, 1, :], Db, Db[:, 1, 2 * cp:2 * cp + 2, :].rearrange("p c n -> p (c n)"), False, True)
                for gq in range(4):
                    y_ = ym[(g % 2) * 4 + gq]
                    ACT(y_, y_[:, :, :].rearrange("p c n -> p (c n)"), pq[2], pq[2][:64, :], AF.Identity, extra=[gmk], scale=gmk[:64, gq:gq + 1])
                    for jb in range(4):
                        r0 = jb * 512 + gq * 128 + cc0
                        P.dma("sp", rsin2, rsin2[r0:r0 + 4, :].rearrange("c (i n) -> i c n", n=256), y_, y_[jb * 16:(jb + 1) * 16, :, :])
            lc.close()

        def phase_attn():
            lc = Local()
            KTh = lc.sb("KTh", [128, L + CTX], BF16)
            Vh = lc.sb("Vh", [128, 130, 128], BF16)
            Qh = lc.sb("Qh", [128, NT], BF16)
            E = [lc.sb("E%d" % i, [128, 512], BF16) for i in range(3)]
            dl = lc.sb("dl", [128, 256], F32)
            pr_ = lc.sb("pr_", [128, 128], F32)
            sm = lc.sb("sm", [128, 4], F32)
            sg_ = lc.sb("sg_", [128, 1], F32)
            rz = lc.sb("rz", [128, 512], F32)
            o0 = lc.sb("o0", [128, 512], F32)
            o1 = lc.sb("o1", [128, 512], F32)
            sqo = lc.sb("sqo", [128, 512], BF16)
            rs_ = lc.sb("rs_", [128, 512], F32)
            yo = [lc.sb("yo%d" % i, [128, 512], BF16) for i in range(2)]
            P.dma("sp", dl, dl[:], dlam, dlam[:])
            P.dma("sp", sg_, sg_[:], subg, subg[:])
            TT("dve", pr_, pr_[:, 0:64], dl, dl[:, 0:64], dl, dl[:, 64:128], ALU.mult)
            TT("dve", pr_, pr_[:, 64:128], dl, dl[:, 128:192], dl, dl[:, 192:256], ALU.mult)
            P.op("dve", "tensor_reduce", reads=[pr_], writes=[sm], out=sm[:, 0:2], in_=pr_[:].rearrange("p (a b) -> p a b", a=2),
                 axis=mybir.AxisListType.X, op=ALU.add)
            ACT(sm, sm[:, 0:2], sm, sm[:, 0:2], AF.Exp)
            TT("dve", sm, sm[:, 2:3], sm, sm[:, 1:2], sm, sm[:, 0:1], ALU.subtract)
            TS("dve", sm, sm[:, 2:3], sm, sm[:, 2:3], -0.2, None, ALU.add)
            TS("dve", sm, sm[:, 3:4], sg_, sg_[:, 0:1], 0.8, None, ALU.mult)
            kav = kall.t.rearrange("(h r c) t -> h c r t", h=4, r=4)
            vav = vall.t.rearrange("(h t p) e -> h p t e", h=4, p=128)
            vcv = vcx.t.rearrange("(t p) e -> p t e", p=128)
            nkt = cfg.get("n_kt", 130)
            for h in range(n_heads):
                hs = slice(h * 128, (h + 1) * 128)
                P.dma("sp", KTh, KTh[:, 0:L].rearrange("p (r t) -> p r t", r=4), kall, kav[h, :, :, :])
                P.dma("sp", KTh, KTh[:, L:L + CTX], kcT, kcT[hs, :])
                for q4 in range(4):
                    P.dma("sp", Vh, Vh[:, q4 * 32:(q4 + 1) * 32, :], vall, vav[h, :, q4 * 32:(q4 + 1) * 32, :])
                P.dma("sp", Vh, Vh[:, 128:130, :], vcx, vcv[:, :, hs])
                P.dma("sp", Qh, Qh[:], qT, qT[hs, :])
                for qt in range(n_qt):
                    qs = slice(qt * 512, (qt + 1) * 512)
                    steps = [(kt, s) for kt in range(nkt) for s in (0, 1)]
                    ns = len(steps)
                    O = [pb[4], pb[5]]
                    Z = [pb[6], pb[7]]

                    def qk(i):
                        kt, s = steps[i]
                        ps_ = pb[i % 3]
                        sl = slice(s * 64, (s + 1) * 64)
                        PE(ps_, ps_[:, :], KTh, KTh[sl, kt * 128:(kt + 1) * 128], Qh, Qh[sl, qs], True, True)

                    qk(0)
                    qk(1)
                    for i in range(ns):
                        kt, s = steps[i]
                        e_ = E[i % 3]
                        ps_ = pb[i % 3]
                        ACT(e_, e_[:], ps_, ps_[:, :], AF.Exp, scale=0.125)
                        if i + 2 < ns:
                            qk(i + 2)
                        PE(O[s], O[s][:, :], Vh, Vh[:, kt, :], e_, e_[:], kt == 0, kt == nkt - 1)
                        PE(Z[s], Z[s][:, :], ones1, ones1[:], e_, e_[:], kt == 0, kt == nkt - 1)
                    P.op("dve", "reciprocal", reads=[Z[0]], writes=[rz], out=rz[:], in_=Z[0][:, :])
                    TT("dve", o0, o0[:], O[0], O[0][:, :], rz, rz[:], ALU.mult)
                    P.op("dve", "reciprocal", reads=[Z[1]], writes=[rz], out=rz[:], in_=Z[1][:, :])
                    TT("dve", o1, o1[:], O[1], O[1][:, :], rz, rz[:], ALU.mult)
                    STT(o0, o0[:], o1, o1[:], sm, sm[:, 2:3], o0, o0[:], ALU.mult, ALU.add)
                    ACT(sqo, sqo[:], o0, o0[:], AF.Square)
                    PE(pb[3], pb[3][:, :], ones128, ones128[:], sqo, sqo[:], True, True)
                    rsqrt_op(rs_, pb[3], 512, 1e-5)
                    y_ = yo[qt % 2]
                    STT(y_, y_[:], o0, o0[:], sm, sm[:, 3:4], rs_, rs_[:], ALU.mult, ALU.mult)
                    P.dma("sp", ydT, ydT[hs, qs], y_, y_[:])
            lc.close()

        def phase_outproj():
            lc = Local()
            Wo = lc.sb("Wo", [128, 8, 1024], BF16)
            P.dma("pool", Wo, Wo[:, :, :], w_out, w_out.t.rearrange("(kc p) n -> p kc n", p=128))
            hb = lc.sb("hb", [128, 4], F32)
            P.dma("sp", hb, hb[:], hbias, hbias[:])
            X = [lc.sb("Xo%d" % i, [128, 8, 512], F32) for i in range(2)]
            ymix = [lc.sb("ymix%d" % i, [128, 8, 512], BF16) for i in range(2)]
            yc = [lc.sb("yc%d" % i, [128, 4, 512], BF16) for i in range(2)]
            x0 = [lc.sb("x0_%d" % i, [128, 4, 512], BF16) for i in range(2)]
            uv = [lc.sb("uv_%d" % i, [128, 4, 512], BF16) for i in range(2)]
            tq = [lc.sb("tq%d" % i, [128, 512], F32) for i in range(2)]
            hv = h1T.t.rearrange("(fc p) n -> p fc n", p=128)
            yav = yall.t.rearrange("(i p) t -> p i t", p=128)
            x0v = x0T.t.rearrange("(i p) t -> p i t", p=128)
            uvv = uvT.t.rearrange("(i p) t -> p i t", p=128)
            ydv = ydT.t.rearrange("(i p) t -> p i t", p=128)
            for it in range(n_lat_tiles):
                ts_ = slice(it * 512, (it + 1) * 512)
                c0 = C_LAT + it * 512
                X_, ym_, yc_, x0_, uv_ = X[it % 2], ymix[it % 2], yc[it % 2], x0[it % 2], uv[it % 2]
                P.dma("sp", X_, X_[:, :, :], h1T, hv[:, :, c0:c0 + 512])
                P.dma("sp", yc_, yc_[:, :, :], yall, yav[:, :, ts_])
                P.dma("sp", x0_, x0_[:, :, :], x0T, x0v[:, :, ts_])
                P.dma("sp", uv_, uv_[:, :, :], uvT, uvv[:, :, ts_])
                P.dma("sp", ym_, ym_[:, 4:8, :], ydT, ydv[:, :, ts_])
                for i in range(4):
                    t_ = tq[i % 2]
                    STT(t_, t_[:], uv_, uv_[:, i, :], hb, hb[:, i:i + 1], yc_, yc_[:, i, :], ALU.mult, ALU.add)
                    TT("pool", ym_, ym_[:, i, :], t_, t_[:], x0_, x0_[:, i, :], ALU.mult)
                for m in range(8):
                    po = pb[m % 2]
                    for kc in range(8):
                        PE(po, po[:, :], Wo, Wo[:, kc, m * 128:(m + 1) * 128], ym_, ym_[:, kc, :], kc == 0, kc == 7)
                    STT(X_, X_[:, m, :], po, po[:, :], md, md[:, 5 * 8 + m, 0:1], X_, X_[:, m, :], ALU.mult, ALU.add)
                P.dma("sp", h1T, hv[:, :, c0:c0 + 512], X_, X_[:, :, :])
            lc.close()

        dbg_outs = []

        def dump(name, src, shape, dt):
            d_ = dout("dbg_" + name, shape, dt)
            P.dma("sp", d_, d_[:], src, src[:])

        phase_mod()
        phase_ffn(0, all_tiles, 0, final=False)
        if stop_after == "ffn0":
            phase_ffn(1, lat_tiles, 2, final=True)
        else:
            phase_inproj()
            if stop_after != "inproj":
                for h_ in range(4):
                    P.cc("AllGather", ALU.bypass, kin, kin[h_ * 128:(h_ + 1) * 128, :], kall, kall[h_ * 512:(h_ + 1) * 512, :], append=True)
                for h_ in range(4):
                    P.cc("AllGather", ALU.bypass, vin, vin[h_ * NT:(h_ + 1) * NT, :], vall, vall[h_ * 4 * NT:(h_ + 1) * 4 * NT, :], append=True)
                P.cc("ReduceScatter", ALU.add, rsin1, rsin1[:, :], uvch, uvch[:, :])
                phase_filter()
                if stop_after != "filter":
                    phase_fft()
                    P.cc("ReduceScatter", ALU.add, rsin2, rsin2[:, :], yall, yall[:, :])
                    P.barrier()
                    if stop_after != "fft":
                        phase_attn()
                        if stop_after != "attn":
                            phase_outproj()
                            phase_ffn(1, lat_tiles, 2, final=True)
        if dbg:
            for name, src, shape, dt in (("h1T", h1T, [D, NAUG], F32), ("x0T", x0T, [512, NT], BF16), ("uvT", uvT, [512, NT], BF16),
                                         ("qT", qT, [512, NT], BF16), ("kin", kin, [512, NT], BF16), ("vin", vin, [4 * NT, 128], BF16),
                                         ("kcT", kcT, [512, CTX], BF16), ("vcx", vcx, [CTX, 512], BF16),
                                         ("kall", kall, [2048, NT], BF16), ("vall", vall, [16 * NT, 128], BF16),
                                         ("uvch", uvch, [128, L], BF16), ("kscr", kscr, [128, NFFT], F32),
                                         ("yall", yall, [512, NT], BF16), ("ydT", ydT, [512, NT], BF16)):
                if name in cfg.get("dumps", ()):
                    dump(name, src, shape, dt)
        P.barrier()
    return nc


def make_inputs(core, inp, consts):
    b, r = core // 4, core % 4
    t0 = r * NT
    x = inp["x"][b]
    xa = np.zeros((NAUG, D), np.float32)
    xa[C_LAT:C_LAT + NT] = x[t0:t0 + NT]
    if r > 0:
        xa[0:2] = x[t0 - 2:t0]
    if r < 3:
        xa[C_HR:C_HR + 2] = x[t0 + NT:t0 + NT + 2]
    xa[C_CTX:] = inp["ctx"][b]
    m = {}
    m["xT"] = np.ascontiguousarray(xa.T)
    cc = np.stack([inp["c"][b], inp["c_ctx"]], axis=0)
    m["c2"] = np.ascontiguousarray(cc.reshape(2, 8, 128).transpose(2, 1, 0))
    m["ada_w"] = np.ascontiguousarray(inp["ada_w"][0])
    m["ada_bT"] = np.ascontiguousarray(inp["ada_b"][0].reshape(72, 128).T)
    m["normg"] = np.ascontiguousarray(inp["norm_g"][0].reshape(3, 8, 128).transpose(2, 0, 1))
    m["finalg"] = np.ascontiguousarray(inp["final_g"].reshape(8, 128).T)
    for i in range(2):
        m["wg%d" % i] = np.ascontiguousarray(inp["ffn_w_gate"][0, i])
        m["wu%d" % i] = np.ascontiguousarray(inp["ffn_w_up"][0, i])
        m["wd%d" % i] = np.ascontiguousarray(inp["ffn_w_down"][0, i])
    m["w_in"] = np.ascontiguousarray(inp["w_in"][0])
    m["w_out"] = np.ascontiguousarray(inp["w_out"][0])
    m["convw"] = np.ascontiguousarray(inp["hyena_conv_w"][0].reshape(3, 12, 128).transpose(2, 1, 0))
    m["convb"] = np.ascontiguousarray(inp["hyena_conv_b"][0].reshape(12, 128).T)
    ml = np.zeros((128, 2), np.float32)
    ml[:, 0] = 1.0 if r > 0 else 0.0
    ml[:, 1] = 1.0 if r < 3 else 0.0
    m["maskLR"] = ml
    tm = np.zeros((128, 4), np.float32)
    tm[:, r] = 1.0
    m["tmask"] = tm
    cos, sin = rope_tables(t0)
    m["ropec"], m["ropes"] = cos, sin
    m["prot"] = consts["prot"]
    m["embT"] = consts["embT"]
    m["decay"] = decay_table(r)
    m["fw1"] = np.ascontiguousarray(inp["filt_w1"][0])
    m["fvec"] = np.ascontiguousarray(np.stack([inp["filt_b1"][0], inp["filt_sin_freq"][0], inp["filt_b_inner"][0, 0], inp["filt_b_inner"][0, 1]], axis=1))
    m["fwi"] = np.ascontiguousarray(inp["filt_w_inner"][0].transpose(1, 0, 2))
    wo = inp["filt_w_out"][0]
    m["fwo"] = np.ascontiguousarray(np.concatenate([wo[:, r * 128:(r + 1) * 128], wo[:, 512 + r * 128:512 + (r + 1) * 128]], axis=1))
    m["hbias"] = np.ascontiguousarray(inp["hyena_bias"][0].reshape(4, 128).T)
    m["dlam"] = np.ascontiguousarray(np.broadcast_to(inp["diff_lambda"][0].reshape(1, 256), (128, 256)))
    m["subg"] = np.ascontiguousarray(inp["diff_subln_g"][0].reshape(128, 1))
    for k_ in ("f128", "tw", "f256", "g256", "itw", "g128"):
        m[k_] = consts[k_]
    return m


CFG = {}


def kernel(**inputs):
    inp = {k: np.asarray(v) for k, v in inputs.items()}
    consts = host_consts()
    nc = build(CFG)
    in_maps = [make_inputs(c, inp, consts) for c in range(8)]
    res = run_bass_kernel_spmd(nc, in_maps, core_ids=list(range(8)))
    out = np.zeros((2, L, D), np.float32)
    for c in range(8):
        b, r = c // 4, c % 4
        out[b, r * NT:(r + 1) * NT] = res.results[c]["outT"].T
    kernel.last = res
    return out
```
